# Optimizing a Trainium2 kernel written in Bass

```python
import math
import jax, jax.numpy as jnp
from jax import lax
import numpy as np

D_MODEL = 1024
BATCH = 2
SEQ = 8192
DEPTH = 4

N_MIXERS = 2
N_GLA_LAYERS = (DEPTH + 1) // 2
N_SB_LAYERS = DEPTH // 2
D_FF = 2816
EPS = 1e-6

GLA_HEADS = 4
GLA_DK = D_MODEL // 2
GLA_DV = D_MODEL
GLA_HK = GLA_DK // GLA_HEADS
GLA_HV = GLA_DV // GLA_HEADS
GLA_GATE_RANK = 16
GLA_GATE_TAU = 16.0
GLA_CHUNK = 64
GLA_IN_COLS = GLA_DK + GLA_DK + GLA_DV + GLA_DV + GLA_GATE_RANK

SB_HEADS = 16
SB_HD = D_MODEL // SB_HEADS
SB_QBLOCK = 128
SB_IN_COLS = 3 * D_MODEL

kernel_name = "hybrid_gla_stickbreaking_macaron"


def rmsnorm(x, g):
    xf = x.astype(jnp.float32)
    y = xf * lax.rsqrt(jnp.mean(xf * xf, axis=-1, keepdims=True) + EPS)
    return (y * g.astype(jnp.float32)).astype(x.dtype)


def swiglu(h, w_gate, w_up, w_down):
    return (jax.nn.silu(h @ w_gate) * (h @ w_up)) @ w_down


def gla_mixer(h, w_in, w_gk2, b_gk, o_norm, w_out):
    B, S, _ = h.shape
    nc = S // GLA_CHUNK
    proj = h @ w_in
    q, k, v, g, gk_lr = jnp.split(
        proj, [GLA_DK, 2 * GLA_DK, 2 * GLA_DK + GLA_DV, 2 * GLA_DK + 2 * GLA_DV], axis=-1)
    log_a = jax.nn.log_sigmoid((gk_lr @ w_gk2 + b_gk).astype(jnp.float32)) / GLA_GATE_TAU

    def to_chunks(t, d):
        return t.astype(jnp.float32).reshape(B, nc, GLA_CHUNK, GLA_HEADS, d).transpose(1, 0, 3, 2, 4)

    qc = to_chunks(q, GLA_HK) * (GLA_HK ** -0.5)
    kc = to_chunks(k, GLA_HK)
    vc = to_chunks(v, GLA_HV)
    bc = jnp.cumsum(to_chunks(log_a, GLA_HK), axis=3)
    causal = jnp.tril(jnp.ones((GLA_CHUNK, GLA_CHUNK), dtype=bool))

    def step(state, inp):
        q_t, k_t, v_t, b_t = inp
        o_inter = jnp.einsum('bhtk,bhkv->bhtv', q_t * jnp.exp(b_t), state)
        diff = b_t[:, :, :, None, :] - b_t[:, :, None, :, :]
        decay = jnp.exp(jnp.where(causal[:, :, None], diff, -jnp.inf))
        scores = jnp.einsum('bhtk,bhtsk,bhsk->bhts', q_t, decay, k_t)
        o_intra = jnp.einsum('bhts,bhsv->bhtv', scores, v_t)
        b_last = b_t[:, :, -1, :]
        k_dec = k_t * jnp.exp(b_last[:, :, None, :] - b_t)
        state = jnp.exp(b_last)[..., None] * state + jnp.einsum('bhsk,bhsv->bhkv', k_dec, v_t)
        return state, o_inter + o_intra

    state0 = jnp.zeros((B, GLA_HEADS, GLA_HK, GLA_HV), jnp.float32)
    _, o = lax.scan(step, state0, (qc, kc, vc, bc))
    o = o.transpose(1, 0, 3, 2, 4).reshape(B, S, GLA_HEADS, GLA_HV)
    o = rmsnorm(o, o_norm).reshape(B, S, GLA_DV)
    o = o.astype(h.dtype) * jax.nn.silu(g)
    return o @ w_out


def sb_mixer(h, w_in, w_out):
    B, S, _ = h.shape
    nb = S // SB_QBLOCK
    q, k, v = jnp.split(h @ w_in, 3, axis=-1)

    def heads(t):
        return t.reshape(B, S, SB_HEADS, SB_HD).transpose(0, 2, 1, 3)

    q, k, v = heads(q), heads(k), heads(v)
    q_blocks = q.reshape(B, SB_HEADS, nb, SB_QBLOCK, SB_HD).transpose(2, 0, 1, 3, 4)
    s_pos = jnp.arange(S)
    scale = 1.0 / math.sqrt(SB_HD)

    def block(args):
        idx, q_blk = args
        z = jnp.einsum('bhtd,bhsd->bhts', q_blk, k).astype(jnp.float32) * scale
        t_pos = idx * SB_QBLOCK + jnp.arange(SB_QBLOCK)
        mask = s_pos[None, :] < t_pos[:, None]
        sp = jnp.where(mask, jax.nn.softplus(z), 0.0)
        rem = lax.cumsum(sp, axis=3, reverse=True) - sp
        weights = jnp.where(mask, jnp.exp(jax.nn.log_sigmoid(z) - rem), 0.0)
        return jnp.einsum('bhts,bhsd->bhtd', weights.astype(v.dtype), v)

    o = lax.map(block, (jnp.arange(nb), q_blocks))
    o = o.transpose(1, 0, 3, 2, 4).reshape(B, S, D_MODEL)
    return o @ w_out


def setup_inputs(seed: int = 0) -> dict:
    key = jax.random.key(seed)
    ks = jax.random.split(key, 24)
    f32 = jnp.float32

    def nrm(k, shape, fan_in, gain=1.0):
        return jax.random.normal(k, shape, f32) * (gain * fan_in ** -0.5)

    def gain(k, shape):
        return 1.0 + 0.02 * jax.random.normal(k, shape, f32)

    return {
        "x": jax.random.normal(ks[0], (BATCH, SEQ, D_MODEL), f32),
        "ffn1_norm": gain(ks[1], (DEPTH, D_MODEL)),
        "ffn1_w_gate": nrm(ks[2], (DEPTH, D_MODEL, D_FF), D_MODEL),
        "ffn1_w_up": nrm(ks[3], (DEPTH, D_MODEL, D_FF), D_MODEL),
        "ffn1_w_down": nrm(ks[4], (DEPTH, D_FF, D_MODEL), D_FF),
        "mix_norm": gain(ks[5], (DEPTH, D_MODEL)),
        "ffn2_norm": gain(ks[6], (DEPTH, D_MODEL)),
        "ffn2_w_gate": nrm(ks[7], (DEPTH, D_MODEL, D_FF), D_MODEL),
        "ffn2_w_up": nrm(ks[8], (DEPTH, D_MODEL, D_FF), D_MODEL),
        "ffn2_w_down": nrm(ks[9], (DEPTH, D_FF, D_MODEL), D_FF),
        "gla_w_in": nrm(ks[10], (N_GLA_LAYERS, D_MODEL, GLA_IN_COLS), D_MODEL),
        "gla_w_gk2": nrm(ks[11], (N_GLA_LAYERS, GLA_GATE_RANK, GLA_DK), GLA_GATE_RANK),
        "gla_b_gk": 0.1 * jax.random.normal(ks[12], (N_GLA_LAYERS, GLA_DK), f32),
        "gla_o_norm": gain(ks[13], (N_GLA_LAYERS, GLA_HV)),
        "gla_w_out": nrm(ks[14], (N_GLA_LAYERS, GLA_DV, D_MODEL), GLA_DV),
        "sb_w_in": nrm(ks[15], (N_SB_LAYERS, D_MODEL, SB_IN_COLS), D_MODEL),
        "sb_w_out": nrm(ks[16], (N_SB_LAYERS, D_MODEL, D_MODEL), D_MODEL),
        "final_norm": gain(ks[17], (D_MODEL,)),
    }


def reference(x, ffn1_norm, ffn1_w_gate, ffn1_w_up, ffn1_w_down, mix_norm,
              ffn2_norm, ffn2_w_gate, ffn2_w_up, ffn2_w_down,
              gla_w_in, gla_w_gk2, gla_b_gk, gla_o_norm, gla_w_out,
              sb_w_in, sb_w_out, final_norm):
    for i in range(DEPTH):
        x = x + 0.5 * swiglu(rmsnorm(x, ffn1_norm[i]), ffn1_w_gate[i], ffn1_w_up[i], ffn1_w_down[i])
        h = rmsnorm(x, mix_norm[i])
        j = i // N_MIXERS
        if i % N_MIXERS == 0:
            y = gla_mixer(h, gla_w_in[j], gla_w_gk2[j], gla_b_gk[j], gla_o_norm[j], gla_w_out[j])
        else:
            y = sb_mixer(h, sb_w_in[j], sb_w_out[j])
        x = x + y.astype(x.dtype)
        x = x + 0.5 * swiglu(rmsnorm(x, ffn2_norm[i]), ffn2_w_gate[i], ffn2_w_up[i], ffn2_w_down[i])
    return rmsnorm(x, final_norm)
```

```python
import numpy as np
import ml_dtypes
from contextlib import ExitStack
import concourse.bass as bass
import concourse.mybir as mybir
from concourse.bass_utils import run_bass_kernel_spmd

F32 = mybir.dt.float32
BF16 = mybir.dt.bfloat16
AF = mybir.ActivationFunctionType
ALU = mybir.AluOpType
NPBF = ml_dtypes.bfloat16

D = 1024
TOK = 2048
SEQ = 8192
DFF = 2816
EPS = 1e-6
ENGS = ("pe", "dve", "act", "pool", "sp")


class Buf:
    __slots__ = ("name", "w", "r")

    def __init__(self, name=""):
        self.name = name
        self.w = None
        self.r = []


class Prog:
    NDMA = 8

    def __init__(self, nc):
        self.nc = nc
        self.ops = {e: [] for e in ENGS}
        self.cnt = {e: 0 for e in ENGS}
        self.dcnt = {e: 0 for e in ENGS}
        self.seen = {e: {} for e in ENGS}

    def _need(self, eng, tok, waits):
        if tok is None:
            return
        key, val = tok
        if self.seen[eng].get(key, 0) >= val:
            return
        self.seen[eng][key] = val
        waits.append(tok)

    def _deps(self, eng, reads, writes):
        waits = []
        for b in reads:
            self._need(eng, b.w, waits)
        for b in writes:
            self._need(eng, b.w, waits)
            for t in b.r:
                self._need(eng, t, waits)
        return waits

    def _commit(self, tok, reads, writes):
        for b in reads:
            b.r.append(tok)
        for b in writes:
            b.w = tok
            b.r = []

    def op(self, eng, emit, reads=(), writes=()):
        waits = self._deps(eng, reads, writes)
        self.cnt[eng] += 1
        tok = (("c", eng), self.cnt[eng])
        self.ops[eng].append((waits, emit, tok))
        self._commit(tok, reads, writes)
        return tok

    def dma(self, eng, emit, reads=(), writes=()):
        waits = self._deps(eng, reads, writes)
        i = self.dcnt[eng]
        self.dcnt[eng] += 1
        k, gen = i % self.NDMA, i // self.NDMA
        key = ("d", eng, k)
        if gen > 0:
            self._need(eng, (key, 16 * gen), waits)
        tok = (key, 16 * (gen + 1))
        self.ops[eng].append((waits, emit, tok))
        self._commit(tok, reads, writes)
        return tok

    def finish_on(self, eng, toks):
        waits = []
        for t in reversed(toks):
            self._need(eng, t, waits)
        self.ops[eng].append((waits, None, None))

    def emit(self):
        nc = self.nc
        with ExitStack() as es:
            sems = {}
            for e in ENGS:
                sems[("c", e)] = es.enter_context(nc.semaphore(f"c_{e}"))
                if self.dcnt[e]:
                    for k in range(self.NDMA):
                        sems[("d", e, k)] = es.enter_context(nc.semaphore(f"d_{e}{k}"))
            block = es.enter_context(nc.Block())

            def run(engobj, e):
                for waits, emit, tok in self.ops[e]:
                    for (key, val) in waits:
                        engobj.wait_ge(sems[key], val)
                    if emit is None:
                        continue
                    ins = emit(engobj)
                    key, val = tok
                    ins.then_inc(sems[key], 1 if key[0] == "c" else 16)

            @block.tensor
            def _(t):
                run(t, "pe")

            @block.vector
            def _(v):
                run(v, "dve")

            @block.scalar
            def _(s):
                run(s, "act")

            @block.gpsimd
            def _(g):
                run(g, "pool")

            @block.sync
            def _(sy):
                run(sy, "sp")


class Ring:
    def __init__(self, aps):
        self.aps = aps
        self.bufs = [Buf() for _ in aps]
        self.i = 0

    def next(self):
        k = self.i % len(self.aps)
        self.i += 1
        return self.aps[k], self.bufs[k]


def mm_group(out, pairs):
    n = len(pairs)

    def emit(e):
        ins = None
        for i, (l, r) in enumerate(pairs):
            ins = e.matmul(out, l, r, start=(i == 0), stop=(i == n - 1))
        return ins
    return emit


def MM(out, l, r, start=True, stop=True):
    return lambda e: e.matmul(out, l, r, start=start, stop=stop)


def ACT(out, in_, func, **kw):
    return lambda e: e.activation(out, in_, func, **kw)


def TT(out, a, b, op):
    return lambda e: e.tensor_tensor(out, a, b, op)


def TS(out, a, s1, s2, op0, op1=None):
    if op1 is None:
        return lambda e: e.tensor_scalar(out, a, s1, s2, op0)
    return lambda e: e.tensor_scalar(out, a, s1, s2, op0, op1)


def STT(out, a, sc, b, op0, op1):
    return lambda e: e.scalar_tensor_tensor(out, a, sc, b, op0, op1)


def CP(out, in_):
    return lambda e: e.tensor_copy(out, in_)


def DMA(out, in_):
    return lambda e: e.dma_start(out=out, in_=in_)


def seq(fns):
    def emit(e):
        ins = None
        for f in fns:
            ins = f(e)
        return ins
    return emit


class KB:
    def __init__(self, name):
        self.nc = bass.Bass("TRN2", target_bir_lowering=False)
        self.P = Prog(self.nc)
        self.es = ExitStack()
        self.name = name
        self.out_toks = []
        self.n_sb = 0

    def din(self, name, shape, dt):
        return self.nc.dram_tensor(name, list(shape), dt, kind="ExternalInput").ap()

    def dout(self, name, shape, dt):
        return self.nc.dram_tensor(name, list(shape), dt, kind="ExternalOutput").ap()

    def sb(self, shape, dt, name=None):
        self.n_sb += 1
        return self.es.enter_context(self.nc.sbuf_tensor("s_" + (name or f"sb{self.n_sb}"), list(shape), dt))

    def ring(self, n, shape, dt, name):
        return Ring([self.sb(shape, dt, f"{name}{i}") for i in range(n)])

    def psum_banks(self):
        self.ps = [self.es.enter_context(self.nc.psum_tensor(f"ps{i}", [128, 512], F32)) for i in range(8)]
        self.psb = [Buf(f"ps{i}") for i in range(8)]
        self.psi = {}

    def ps_next(self, kind, banks):
        i = self.psi.get(kind, 0)
        self.psi[kind] = i + 1
        b = banks[i % len(banks)]
        return self.ps[b], self.psb[b]

    def finish(self):
        self.P.finish_on("sp", self.out_toks)
        self.P.emit()
        self.es.close()
        return self.nc

    def setup_tok(self, n_gcols):
        P = self.P
        self.psum_banks()
        self.xT = self.sb([128, 8, TOK], F32, "xT")
        self.xB = [[Buf(f"x{c}_{t}") for t in range(4)] for c in range(8)]
        self.hT = self.sb([128, 8, TOK], BF16, "hT")
        self.hB = [Buf(f"h{t}") for t in range(4)]
        self.actT = self.sb([128, 4, TOK], BF16, "actT")
        self.aB = [[Buf() for t in range(4)] for c in range(4)]
        self.wA = self.ring(5, [128, 8, 256], BF16, "wA")
        self.wD = self.ring(4, [128, 2, 1024], BF16, "wD")
        self.sq = self.ring(1, [128, 8, 512], BF16, "sq")
        self.rstd = self.ring(2, [128, 512], F32, "rstd")
        self.sg = self.ring(3, [128, 512], F32, "sg")
        self.stg = self.ring(4, [128, 512], BF16, "stg")
        self.stg32r = self.ring(4, [128, 512], F32, "stg32r")
        self.gcols = self.sb([128, n_gcols * 8], F32, "gcols")
        self.gB = Buf("gcols")
        self.ones = self.sb([128, 128], BF16, "ones")
        self.onesB = Buf("ones")
        self.epsT = self.sb([128, 1], F32, "epsT")
        P.op("pool", lambda e: e.memset(self.ones[:], 1.0 / D), writes=[self.onesB])
        P.op("pool", lambda e: e.memset(self.epsT[:], EPS), writes=[self.onesB])
        gc = self.din("gcols_in", [128, n_gcols * 8], F32)
        P.dma("sp", DMA(self.gcols[:], gc[:, :]), writes=[self.gB])

    def load_x(self, x_dram):
        for c in range(8):
            self.P.dma("sp", DMA(self.xT[:, c, :], x_dram[c * 128:(c + 1) * 128, :]), writes=self.xB[c])

    def store_x(self, x_dram):
        for c in range(8):
            self.out_toks.append(self.P.dma("sp", DMA(x_dram[c * 128:(c + 1) * 128, :], self.xT[:, c, :]), reads=self.xB[c]))

    def load_wA(self, W, c0, ncols):
        slot, b = self.wA.next()
        src = W[:, c0:c0 + ncols].rearrange("(k p) n -> p k n", p=128)
        self.P.dma("pool", DMA(slot[:, :, 0:ncols], src), writes=[b])
        return slot, b

    def load_wD(self, W, r0):
        slot, b = self.wD.next()
        src = W[r0:r0 + 256, :].rearrange("(j p) n -> p j n", p=128)
        self.P.dma("pool", DMA(slot[:], src), writes=[b])
        return slot, b

    def norm_to(self, gi, t, dsts, n_feat=D):
        P = self.P
        ts = slice(t * 512, (t + 1) * 512)
        sq, sqb = self.sq.next()
        xbs = [self.xB[c][t] for c in range(8)]
        P.op("act", seq([ACT(sq[:, c, :], self.xT[:, c, ts], AF.Square) for c in range(8)]), reads=xbs, writes=[sqb])
        ss, ssb = self.ps_next("ss", [6])
        P.op("pe", mm_group(ss[:], [(self.ones[:], sq[:, c, :]) for c in range(8)]), reads=[sqb, self.onesB], writes=[ssb])
        rs, rsb = self.rstd.next()
        P.op("act", ACT(rs[:], ss[:], AF.Ln, bias=self.epsT[:, 0:1]), reads=[ssb, self.onesB], writes=[rsb])
        P.op("act", ACT(rs[:], rs[:], AF.Exp, scale=-0.5), reads=[rsb], writes=[rsb])
        fns = [STT(dsts[c][0], self.xT[:, c, ts], self.gcols[:, gi * 8 + c:gi * 8 + c + 1], rs[:], ALU.mult, ALU.mult)
               for c in range(8)]
        if all(d[1] is dsts[0][1] for d in dsts):
            P.op("dve", seq(fns), reads=xbs + [rsb, self.gB], writes=[dsts[0][1]])
        else:
            for c in range(8):
                P.op("dve", fns[c], reads=[xbs[c], rsb, self.gB], writes=[dsts[c][1]])

    def norm_hT(self, gi):
        for t in range(4):
            ts = slice(t * 512, (t + 1) * 512)
            self.norm_to(gi, t, [(self.hT[:, c, ts], self.hB[t]) for c in range(8)])

    def ffn(self, gi, Wg, Wu, Wd):
        P = self.P
        self.norm_hT(gi)
        groups = [[0, 1], [2, 3], [4, 5], [6, 7], [8, 9], [10]]
        for grp in groups:
            for pi, pr in enumerate(grp):
                wg, bg = self.load_wA(Wg, pr * 256, 256)
                wu, bu = self.load_wA(Wu, pr * 256, 256)
                for cc in range(2):
                    cl = pi * 2 + cc
                    cs = slice(cc * 128, (cc + 1) * 128)
                    for t in range(4):
                        ts = slice(t * 512, (t + 1) * 512)
                        psg, bpg = self.ps_next("g", [0, 1])
                        psu, bpu = self.ps_next("u", [2, 3])
                        P.op("pe", mm_group(psg[:], [(wg[:, k, cs], self.hT[:, k, ts]) for k in range(8)]),
                             reads=[bg, self.hB[t]], writes=[bpg])
                        P.op("pe", mm_group(psu[:], [(wu[:, k, cs], self.hT[:, k, ts]) for k in range(8)]),
                             reads=[bu, self.hB[t]], writes=[bpu])
                        sg, bsg = self.sg.next()
                        P.op("act", ACT(sg[:], psg[:], AF.Silu), reads=[bpg], writes=[bsg])
                        P.op("dve", TT(self.actT[:, cl, ts], sg[:], psu[:], ALU.mult), reads=[bsg, bpu], writes=[self.aB[cl][t]])
            wds = [self.load_wD(Wd, pr * 256) for pr in grp]
            n = len(grp) * 2
            for t in range(4):
                ts = slice(t * 512, (t + 1) * 512)
                for dc in range(8):
                    ds_ = slice(dc * 128, (dc + 1) * 128)
                    pso, bpo = self.ps_next("o", [4, 5])
                    pairs = []
                    for i, (wd, bwd) in enumerate(wds):
                        for j in range(2):
                            pairs.append((wd[:, j, ds_], self.actT[:, i * 2 + j, ts]))
                    P.op("pe", mm_group(pso[:], pairs),
                         reads=[b for _, b in wds] + [self.aB[cl][t] for cl in range(n)], writes=[bpo])
                    P.op("dve", STT(self.xT[:, dc, ts], pso[:], 0.5, self.xT[:, dc, ts], ALU.mult, ALU.add),
                         reads=[bpo, self.xB[dc][t]], writes=[self.xB[dc][t]])

    def proj_fm(self, wt, bw, cc, t):
        ts = slice(t * 512, (t + 1) * 512)
        cs = slice(cc * 128, (cc + 1) * 128)
        ps, bp = self.ps_next("g", [0, 1])
        self.P.op("pe", mm_group(ps[:], [(wt[:, k, cs], self.hT[:, k, ts]) for k in range(8)]),
                  reads=[bw, self.hB[t]], writes=[bp])
        return ps, bp

    def proj_tm(self, wt, bw, tb, ncols=256):
        t = tb // 4
        tbs = slice(tb * 128, (tb + 1) * 128)
        ps, bp = self.ps_next("u", [2, 3])
        self.P.op("pe", mm_group(ps[:, 0:ncols], [(self.hT[:, k, tbs], wt[:, k, 0:ncols]) for k in range(8)]),
                  reads=[bw, self.hB[t]], writes=[bp])
        return ps, bp

    def evac_store(self, ps_ap, bp, dram_ap, scale=None, eng="dve"):
        P = self.P
        st, bs = self.stg.next()
        sh = ps_ap.shape
        sv = st[0:sh[0], 0:sh[1]]
        if scale is None:
            if eng == "act":
                P.op("act", lambda e: e.copy(sv, ps_ap), reads=[bp], writes=[bs])
            else:
                P.op("dve", CP(sv, ps_ap), reads=[bp], writes=[bs])
        else:
            P.op("dve", TS(sv, ps_ap, scale, None, ALU.mult), reads=[bp], writes=[bs])
        self.out_toks.append(P.dma("sp", DMA(dram_ap, sv), reads=[bs]))

    def inproj_sb(self, gi, W):
        oq = self.dout("oq", [D, TOK], BF16)
        ok = self.dout("ok", [D, TOK], BF16)
        ov = self.dout("ov", [TOK, D], BF16)
        self.norm_hT(gi)
        for j in range(4):
            for kind, dst, scale in ((0, oq, 0.125), (1, ok, None)):
                wt, bw = self.load_wA(W, kind * D + j * 256, 256)
                for cc in range(2):
                    r0 = j * 256 + cc * 128
                    for t in range(4):
                        ps, bp = self.proj_fm(wt, bw, cc, t)
                        self.evac_store(ps[:], bp, dst[r0:r0 + 128, t * 512:(t + 1) * 512], scale)
            wt, bw = self.load_wA(W, 2 * D + j * 256, 256)
            for tb in range(16):
                ps, bp = self.proj_tm(wt, bw, tb)
                self.evac_store(ps[:, 0:256], bp, ov[tb * 128:(tb + 1) * 128, j * 256:(j + 1) * 256], None, eng="act")

    def outproj(self, W, gate_dram=None):
        P = self.P
        oin = self.din("oin", [D, TOK], BF16)
        for c in range(8):
            P.dma("sp", DMA(self.hT[:, c, :], oin[c * 128:(c + 1) * 128, :]), writes=self.hB)
        if gate_dram is not None:
            for c in range(8):
                for t in range(4):
                    ts = slice(t * 512, (t + 1) * 512)
                    st, bs = self.stg.next()
                    P.dma("sp", DMA(st[:], gate_dram[c * 128:(c + 1) * 128, ts]), writes=[bs])
                    P.op("dve", TT(self.hT[:, c, ts], self.hT[:, c, ts], st[:], ALU.mult),
                         reads=[bs, self.hB[t]], writes=[self.hB[t]])
        for j in range(4):
            wt, bw = self.load_wA(W, j * 256, 256)
            for cc in range(2):
                dc = j * 2 + cc
                for t in range(4):
                    ts = slice(t * 512, (t + 1) * 512)
                    ps, bp = self.proj_fm(wt, bw, cc, t)
                    P.op("dve", TT(self.xT[:, dc, ts], ps[:], self.xT[:, dc, ts], ALU.add),
                         reads=[bp, self.xB[dc][t]], writes=[self.xB[dc][t]])

    def final_norm(self, gi, out_dram):
        for t in range(4):
            ts = slice(t * 512, (t + 1) * 512)
            slots = [self.stg32r.next() for c in range(4)]
            dsts = [(slots[c % 4][0][:], Buf()) for c in range(8)]
            P = self.P
            sq, sqb = self.sq.next()
            xbs = [self.xB[c][t] for c in range(8)]
            P.op("act", seq([ACT(sq[:, c, :], self.xT[:, c, ts], AF.Square) for c in range(8)]), reads=xbs, writes=[sqb])
            ss, ssb = self.ps_next("ss", [6])
            P.op("pe", mm_group(ss[:], [(self.ones[:], sq[:, c, :]) for c in range(8)]), reads=[sqb, self.onesB], writes=[ssb])
            rs, rsb = self.rstd.next()
            P.op("act", ACT(rs[:], ss[:], AF.Ln, bias=self.epsT[:, 0:1]), reads=[ssb, self.onesB], writes=[rsb])
            P.op("act", ACT(rs[:], rs[:], AF.Exp, scale=-0.5), reads=[rsb], writes=[rsb])
            for c in range(8):
                st, bs = self.stg32r.next()
                P.op("dve", STT(st[:], self.xT[:, c, ts], self.gcols[:, gi * 8 + c:gi * 8 + c + 1], rs[:], ALU.mult, ALU.mult),
                     reads=[xbs[c], rsb, self.gB], writes=[bs])
                self.out_toks.append(P.dma("sp", DMA(out_dram[c * 128:(c + 1) * 128, ts], st[:]), reads=[bs]))

    def inproj_gla(self, gi, W, Wgk2aug):
        P = self.P
        oq = self.dout("gq", [512, TOK], BF16)
        okT = self.dout("gkT", [512, TOK], BF16)
        okk = self.dout("gk", [TOK, 512], BF16)
        ov = self.dout("gv", [TOK, D], BF16)
        osg = self.dout("gsg", [D, TOK], BF16)
        ola = self.dout("gla", [TOK, 512], F32)
        self.norm_hT(gi)
        qs = 128 ** -0.5
        for j in range(2):
            wt, bw = self.load_wA(W, j * 256, 256)
            for cc in range(2):
                r0 = j * 256 + cc * 128
                for t in range(4):
                    ps, bp = self.proj_fm(wt, bw, cc, t)
                    self.evac_store(ps[:], bp, oq[r0:r0 + 128, t * 512:(t + 1) * 512], qs)
        for j in range(2):
            wt, bw = self.load_wA(W, 512 + j * 256, 256)
            for cc in range(2):
                r0 = j * 256 + cc * 128
                for t in range(4):
                    ps, bp = self.proj_fm(wt, bw, cc, t)
                    self.evac_store(ps[:], bp, okT[r0:r0 + 128, t * 512:(t + 1) * 512], None)
            for tb in range(16):
                ps, bp = self.proj_tm(wt, bw, tb)
                self.evac_store(ps[:, 0:256], bp, okk[tb * 128:(tb + 1) * 128, j * 256:(j + 1) * 256], None, eng="act")
        for j in range(4):
            wt, bw = self.load_wA(W, 1024 + j * 256, 256)
            for tb in range(16):
                ps, bp = self.proj_tm(wt, bw, tb)
                self.evac_store(ps[:, 0:256], bp, ov[tb * 128:(tb + 1) * 128, j * 256:(j + 1) * 256], None, eng="act")
        gkT = self.sb([32, TOK], F32, "gkT")
        gkB = Buf("gkT")
        w2 = self.sb([32, 512], F32, "w2")
        w2B = Buf("w2")
        P.op("pool", lambda e: e.memset(gkT[:], 1.0), writes=[gkB])
        P.dma("sp", DMA(w2[0:17, :], Wgk2aug[:, :]), writes=[w2B])
        wt, bw = self.load_wA(W, 3072, 16)
        for t in range(4):
            ts = slice(t * 512, (t + 1) * 512)
            ps, bp = self.ps_next("g", [0, 1])
            P.op("pe", mm_group(ps[0:16, :], [(wt[:, k, 0:16], self.hT[:, k, ts]) for k in range(8)]),
                 reads=[bw, self.hB[t]], writes=[bp])
            P.op("dve", CP(gkT[0:16, ts], ps[0:16, :]), reads=[bp], writes=[gkB])
        for tb in range(16):
            tbs = slice(tb * 128, (tb + 1) * 128)
            ps, bp = self.ps_next("u", [2, 3])
            P.op("pe", MM(ps[:], gkT[0:17, tbs], w2[0:17, :]), reads=[gkB, w2B], writes=[bp])
            s1, b1 = self.stg32r.next()
            P.op("act", ACT(s1[:], ps[:], AF.Exp, scale=-1.0), reads=[bp], writes=[b1])
            P.op("act", ACT(s1[:], s1[:], AF.Ln, bias=1.0), reads=[b1], writes=[b1])
            P.op("dve", TS(s1[:], s1[:], -1.0 / 16.0, None, ALU.mult), reads=[b1], writes=[b1])
            self.out_toks.append(P.dma("sp", DMA(ola[tbs, :], s1[:]), reads=[b1]))
        for j in range(4):
            wt, bw = self.load_wA(W, 2048 + j * 256, 256)
            for cc in range(2):
                r0 = j * 256 + cc * 128
                for t in range(4):
                    ps, bp = self.proj_fm(wt, bw, cc, t)
                    st, bs = self.stg.next()
                    P.op("act", ACT(st[:], ps[:], AF.Silu), reads=[bp], writes=[bs])
                    self.out_toks.append(P.dma("sp", DMA(osg[r0:r0 + 128, t * 512:(t + 1) * 512], st[:]), reads=[bs]))


def build_tok(phases):
    kb = KB("tok")
    kb.setup_tok(4)
    x_in = kb.din("x_in", [D, TOK], F32)
    kb.load_x(x_in)
    for ph in phases:
        if ph == "ffn1" or ph == "ffn2":
            Wg = kb.din(ph + "_g", [D, DFF], F32)
            Wu = kb.din(ph + "_u", [D, DFF], F32)
            Wd = kb.din(ph + "_d", [DFF, D], F32)
            kb.ffn(0 if ph == "ffn1" else 2, Wg, Wu, Wd)
        elif ph == "in_sb":
            kb.inproj_sb(1, kb.din("w_in", [D, 3 * D], F32))
        elif ph == "in_gla":
            kb.inproj_gla(1, kb.din("w_in", [D, 3088], F32), kb.din("w_gk2aug", [17, 512], F32))
        elif ph == "out_sb":
            kb.outproj(kb.din("w_out", [D, D], F32))
        elif ph == "out_gla":
            kb.outproj(kb.din("w_out", [D, D], F32), gate_dram=kb.din("sg_in", [D, TOK], BF16))
        elif ph == "final":
            kb.final_norm(3, kb.dout("y", [D, TOK], F32))
    if "final" not in phases:
        kb.store_x(kb.dout("x_out", [D, TOK], F32))
    return kb.finish()


def build_sb():
    kb = KB("sb")
    P = kb.P
    kb.psum_banks()
    qT_d = kb.din("qT", [256, SEQ], BF16)
    kT_d = kb.din("kT", [256, SEQ], BF16)
    v_d = kb.din("v", [SEQ, 256], BF16)
    msk_d = kb.din("masks", [4, 128, 512], BF16)
    tri_d = kb.din("tri", [128, 128], BF16)
    o_d = kb.dout("oT", [256, SEQ], BF16)
    QT = kb.sb([128, 2, SEQ], BF16, "QT")
    KT = kb.sb([128, 2, SEQ], BF16, "KT")
    KN = kb.sb([128, 2, SEQ], BF16, "KN")
    V = kb.sb([128, 64, 256], BF16, "V")
    M = kb.sb([128, 4, 512], BF16, "M")
    tri = kb.sb([128, 128], BF16, "tri")
    ones = kb.sb([128, 128], BF16, "ones")
    cB = Buf("consts")
    qB = [Buf() for _ in range(2)]
    kBf = [Buf() for _ in range(2)]
    knB = [Buf() for _ in range(2)]
    vB = Buf("V")
    P.op("pool", lambda e: e.memset(ones[:], 1.0), writes=[cB])
    P.dma("sp", DMA(tri[:], tri_d[:, :]), writes=[cB])
    P.dma("sp", DMA(M[:], msk_d.rearrange("j p t -> p j t")), writes=[cB])
    for c in range(2):
        P.dma("sp", DMA(QT[:, c, :], qT_d[c * 128:(c + 1) * 128, :]), writes=[qB[c]])
        P.dma("sp", DMA(KT[:, c, :], kT_d[c * 128:(c + 1) * 128, :]), writes=[kBf[c]])
    for q4 in range(4):
        P.dma("sp", DMA(V[:, q4 * 16:(q4 + 1) * 16, :],
                        v_d[q4 * 2048:(q4 + 1) * 2048, :].rearrange("(b p) c -> p b c", p=128)), writes=[vB])
    for c in range(2):
        for q4 in range(4):
            sl = slice(q4 * 2048, (q4 + 1) * 2048)
            P.op("pool", TS(KN[:, c, sl], KT[:, c, sl], -1.0, None, ALU.mult), reads=[kBf[c]], writes=[knB[c]])
    e_r = kb.ring(3, [128, 512], F32, "e")
    sp_r = kb.ring(3, [128, 512], BF16, "sp")
    w_r = kb.ring(3, [128, 512], BF16, "w")
    acc_r = kb.ring(2, [128, 512], F32, "acc")
    accb_r = kb.ring(2, [128, 512], BF16, "accb")
    ost = kb.ring(2, [64, 512], BF16, "ost")
    for qt in range(16):
        ts = slice(qt * 512, (qt + 1) * 512)
        for h in range(4):
            c, r = h // 2, (h % 2) * 64
            rs = slice(r, r + 64)
            pso, bpo = kb.ps_next("o", [4, 5])
            acc, baf = acc_r.next()
            accb, bab = accb_r.next()
            kbs = list(range(4 * qt + 3, -1, -1))
            nk = len(kbs)
            for i, kbk in enumerate(kbs):
                ks = slice(kbk * 128, (kbk + 1) * 128)
                diag = kbk >= 4 * qt
                j = kbk - 4 * qt
                psz, bpz = kb.ps_next("z", [0, 1])
                P.op("pe", MM(psz[:], KT[rs, c, ks], QT[rs, c, ts]), reads=[kBf[c], qB[c]], writes=[bpz])
                ee, be = e_r.next()
                P.op("act", ACT(ee[:], psz[:], AF.Exp), reads=[bpz], writes=[be])
                sp, bsp = sp_r.next()
                P.op("act", ACT(sp[:], ee[:], AF.Ln, bias=1.0), reads=[be], writes=[bsp])
                if diag:
                    P.op("dve", TT(sp[:], sp[:], M[:, j, :], ALU.mult), reads=[bsp, cB], writes=[bsp])
                psr, bpr = kb.ps_next("r", [2, 3])
                pairs = [(KN[rs, c, ks], QT[rs, c, ts]), (tri[:], sp[:])]
                rd = [knB[c], qB[c], cB, bsp]
                if i > 0:
                    pairs.append((ones[:], accb[:]))
                    rd.append(bab)
                P.op("pe", mm_group(psr[:], pairs), reads=rd, writes=[bpr])
                w, bw = w_r.next()
                P.op("act", ACT(w[:], psr[:], AF.Exp, scale=-1.0), reads=[bpr], writes=[bw])
                if diag:
                    P.op("dve", TT(w[:], w[:], M[:, j, :], ALU.mult), reads=[bw, cB], writes=[bw])
                if i < nk - 1:
                    if i == 0:
                        P.op("pool", CP(acc[:], sp[:]), reads=[bsp], writes=[baf])
                    else:
                        P.op("pool", TT(acc[:], acc[:], sp[:], ALU.add), reads=[bsp, baf], writes=[baf])
                    P.op("pool", CP(accb[:], acc[:]), reads=[baf], writes=[bab])
                P.op("pe", MM(pso[0:64, :], V[:, kbk, h * 64:(h + 1) * 64], w[:], start=(i == 0), stop=(i == nk - 1)),
                     reads=[vB, bw], writes=[bpo])
            st, bs = ost.next()
            P.op("dve", CP(st[:], pso[0:64, :]), reads=[bpo], writes=[bs])
            kb.out_toks.append(P.dma("sp", DMA(o_d[h * 64:(h + 1) * 64, ts], st[:]), reads=[bs]))
    return kb.finish()


def build_gla():
    kb = KB("gla")
    P = kb.P
    kb.psum_banks()
    qT_d = kb.din("qT", [128, SEQ], BF16)
    kT_d = kb.din("kT", [128, SEQ], BF16)
    k_d = kb.din("k", [SEQ, 128], BF16)
    v_d = kb.din("v", [SEQ, 256], BF16)
    la_d = kb.din("la", [SEQ, 128], F32)
    cst_d = kb.din("gcst", [3, 128, 128], F32)
    on_d = kb.din("onorm", [128, 2], F32)
    o_d = kb.dout("oT", [256, SEQ], BF16)
    QT = kb.sb([128, SEQ], BF16, "QT")
    KT = kb.sb([128, SEQ], BF16, "KT")
    KK = kb.sb([128, 64, 128], BF16, "KK")
    V = kb.sb([128, 64, 256], BF16, "V")
    LA = kb.sb([128, 64, 128], F32, "LA")
    CST = kb.sb([128, 3, 128], F32, "CST")
    ON = kb.sb([128, 2], F32, "ON")
    ones = kb.sb([128, 128], BF16, "ones")
    S = kb.sb([128, 256], F32, "S")
    Sb = kb.sb([128, 256], BF16, "Sb")
    inB = Buf("in")
    cB = Buf("c")
    SB_, SbB = Buf("S"), Buf("Sb")
    epsT = kb.sb([128, 1], F32, "epsT")
    P.op("pool", lambda e: e.memset(ones[:], 1.0 / 256), writes=[cB])
    P.op("pool", lambda e: e.memset(epsT[:], EPS), writes=[cB])
    P.op("pool", lambda e: e.memset(S[:], 0.0), writes=[SB_])
    P.op("pool", lambda e: e.memset(Sb[:], 0.0), writes=[SbB])
    P.dma("sp", DMA(CST[:], cst_d.rearrange("j p t -> p j t")), writes=[cB])
    P.dma("sp", DMA(ON[:], on_d[:, :]), writes=[cB])
    P.dma("sp", DMA(QT[:], qT_d[:, :]), writes=[inB])
    P.dma("sp", DMA(KT[:], kT_d[:, :]), writes=[inB])
    for q4 in range(4):
        rsl = slice(q4 * 2048, (q4 + 1) * 2048)
        bsl = slice(q4 * 16, (q4 + 1) * 16)
        P.dma("sp", DMA(KK[:, bsl, :], k_d[rsl, :].rearrange("(b p) c -> p b c", p=128)), writes=[inB])
        P.dma("sp", DMA(V[:, bsl, :], v_d[rsl, :].rearrange("(b p) c -> p b c", p=128)), writes=[inB])
        P.dma("sp", DMA(LA[:, bsl, :], la_d[rsl, :].rearrange("(b p) c -> p b c", p=128)), writes=[inB])
    eb_r = kb.ring(2, [128, 128], F32, "eb")
    enb_r = kb.ring(2, [128, 128], F32, "enb")
    erb_r = kb.ring(2, [128, 128], F32, "erb")
    qs_r = kb.ring(2, [128, 128], BF16, "qs")
    ks_r = kb.ring(2, [128, 128], BF16, "ks")
    kd_r = kb.ring(2, [128, 128], BF16, "kd")
    scm_r = kb.ring(2, [128, 128], BF16, "scm")
    sq_r = kb.ring(2, [128, 2, 128], BF16, "sq")
    rs_r = kb.ring(2, [128, 128], F32, "rs")
    ost = kb.ring(4, [128, 128], BF16, "ost")
    for tb in range(64):
        tsl = slice(tb * 128, (tb + 1) * 128)
        psb, bpb = kb.ps_next("b", [0, 1])
        P.op("pe", MM(psb[:, 0:128], LA[:, tb, :], CST[:, 0, :]), reads=[inB, cB], writes=[bpb])
        psrb, bprb = kb.ps_next("rb", [2, 3])
        P.op("pe", MM(psrb[:, 0:128], CST[:, 1, :], LA[:, tb, :]), reads=[inB, cB], writes=[bprb])
        eb, beb = eb_r.next()
        enb, benb = enb_r.next()
        erb, berb = erb_r.next()
        P.op("act", ACT(eb[:], psb[:, 0:128], AF.Exp), reads=[bpb], writes=[beb])
        P.op("act", ACT(enb[:], psb[:, 0:128], AF.Exp, scale=-1.0), reads=[bpb], writes=[benb])
        P.op("act", ACT(erb[:], psrb[:, 0:128], AF.Exp), reads=[bprb], writes=[berb])
        qs, bqs = qs_r.next()
        ks, bks = ks_r.next()
        kd, bkd = kd_r.next()
        P.op("dve", TT(qs[:], QT[:, tsl], eb[:], ALU.mult), reads=[inB, beb], writes=[bqs])
        P.op("dve", TT(ks[:], KT[:, tsl], enb[:], ALU.mult), reads=[inB, benb], writes=[bks])
        P.op("dve", TT(kd[:], KK[:, tb, :], erb[:], ALU.mult), reads=[inB, berb], writes=[bkd])
        pss, bpss = kb.ps_next("sc", [6])
        P.op("pe", MM(pss[:, 0:128], ks[:], qs[:]), reads=[bks, bqs], writes=[bpss])
        scm, bscm = scm_r.next()
        P.op("dve", TT(scm[:], pss[:, 0:128], CST[:, 2, :], ALU.mult), reads=[bpss, cB], writes=[bscm])
        pso, bpo = kb.ps_next("o", [4, 5])
        for ch in range(2):
            csl = slice(ch * 64, (ch + 1) * 64)
            fns = []
            for vh in range(2):
                vs = slice(vh * 128, (vh + 1) * 128)
                osl = slice(vh * 128 + ch * 64, vh * 128 + ch * 64 + 64)
                fns.append(MM(pso[:, osl], V[:, tb, vs], scm[:, csl], start=True, stop=False))
                fns.append(MM(pso[:, osl], Sb[:, vs], qs[:, csl], start=False, stop=True))
            P.op("pe", seq(fns), reads=[inB, bscm, SbB, bqs], writes=[bpo])
            psS, bpS = kb.ps_next("S", [7])
            P.op("pe", MM(psS[:, 0:256], kd[csl, :], V[csl, tb, :]), reads=[bkd, inB], writes=[bpS])
            dcol = ch * 64 + 63
            P.op("dve", STT(S[:], S[:], eb[:, dcol:dcol + 1], psS[:, 0:256], ALU.mult, ALU.add),
                 reads=[SB_, beb, bpS], writes=[SB_])
            P.op("act", lambda e: e.copy(Sb[:], S[:]), reads=[SB_], writes=[SbB])
        sq, bsq = sq_r.next()
        P.op("act", seq([ACT(sq[:, vh, :], pso[:, vh * 128:(vh + 1) * 128], AF.Square) for vh in range(2)]),
             reads=[bpo], writes=[bsq])
        psn, bpn = kb.ps_next("n", [6])
        P.op("pe", mm_group(psn[:, 128:256], [(ones[:], sq[:, 0, :]), (ones[:], sq[:, 1, :])]), reads=[bsq, cB], writes=[bpn])
        rs, brs = rs_r.next()
        P.op("act", ACT(rs[:], psn[:, 128:256], AF.Ln, bias=epsT[:, 0:1]), reads=[bpn, cB], writes=[brs])
        P.op("act", ACT(rs[:], rs[:], AF.Exp, scale=-0.5), reads=[brs], writes=[brs])
        for vh in range(2):
            st, bs = ost.next()
            P.op("dve", STT(st[:], pso[:, vh * 128:(vh + 1) * 128], ON[:, vh:vh + 1], rs[:], ALU.mult, ALU.mult),
                 reads=[bpo, brs, cB], writes=[bs])
            kb.out_toks.append(P.dma("sp", DMA(o_d[vh * 128:(vh + 1) * 128, tsl], st[:]), reads=[bs]))
    return kb.finish()


_PROGS = {}


def _prog(key, fn):
    if key not in _PROGS:
        _PROGS[key] = fn()
    return _PROGS[key]


def _run(nc, in_maps):
    res = run_bass_kernel_spmd(nc, in_maps, core_ids=list(range(8)))
    return res.results


def _cols(v):
    return np.ascontiguousarray(np.asarray(v, np.float32).reshape(8, 128).T)


def _sb_consts():
    s = np.arange(128)[:, None]
    t = np.arange(512)[None, :]
    masks = np.stack([(t > (128 * j + s)) for j in range(4)]).astype(np.float32).astype(NPBF)
    tri = (np.arange(128)[:, None] >= np.arange(128)[None, :]).astype(np.float32).astype(NPBF)
    return masks, tri


def _gla_consts():
    i = np.arange(128)
    same = (i[:, None] // 64) == (i[None, :] // 64)
    tinc = (same & (i[:, None] <= i[None, :])).astype(np.float32)
    tstr = (same & (i[:, None] > i[None, :])).astype(np.float32)
    return np.ascontiguousarray(np.stack([tinc, tstr, tinc]))


def kernel(x, ffn1_norm, ffn1_w_gate, ffn1_w_up, ffn1_w_down, mix_norm,
           ffn2_norm, ffn2_w_gate, ffn2_w_up, ffn2_w_down,
           gla_w_in, gla_w_gk2, gla_b_gk, gla_o_norm, gla_w_out,
           sb_w_in, sb_w_out, final_norm):
    f32 = lambda a: np.ascontiguousarray(np.asarray(a, np.float32))
    x = f32(x)
    xs = [np.ascontiguousarray(x[c // 4, (c % 4) * TOK:(c % 4 + 1) * TOK, :].T) for c in range(8)]
    masks, tri = _sb_consts()
    gcst = _gla_consts()
    mixer_out = None
    sg = None
    y = None
    for i in range(4):
        j = i // 2
        gla = (i % 2 == 0)
        phases = []
        m = {}
        if i > 0:
            pg = ((i - 1) % 2 == 0)
            phases += ["out_gla" if pg else "out_sb", "ffn2"]
            m["w_out"] = f32(gla_w_out[(i - 1) // 2] if pg else sb_w_out[(i - 1) // 2])
            m["ffn2_g"], m["ffn2_u"], m["ffn2_d"] = f32(ffn2_w_gate[i - 1]), f32(ffn2_w_up[i - 1]), f32(ffn2_w_down[i - 1])
        phases += ["ffn1", "in_gla" if gla else "in_sb"]
        m["ffn1_g"], m["ffn1_u"], m["ffn1_d"] = f32(ffn1_w_gate[i]), f32(ffn1_w_up[i]), f32(ffn1_w_down[i])
        if gla:
            m["w_in"] = f32(gla_w_in[j])
            m["w_gk2aug"] = np.ascontiguousarray(np.concatenate([f32(gla_w_gk2[j]), f32(gla_b_gk[j])[None, :]], 0))
        else:
            m["w_in"] = f32(sb_w_in[j])
        gc = [_cols(ffn1_norm[i]), _cols(mix_norm[i]), _cols(ffn2_norm[i - 1] if i > 0 else ffn2_norm[0]), _cols(final_norm)]
        m["gcols_in"] = np.ascontiguousarray(np.concatenate(gc, 1))
        nc = _prog(tuple(phases), lambda: build_tok(phases))
        in_maps = []
        for c in range(8):
            mm = dict(m)
            mm["x_in"] = xs[c]
            if i > 0:
                mm["oin"] = mixer_out[c]
                if "out_gla" in phases:
                    mm["sg_in"] = sg[c]
            in_maps.append(mm)
        res = _run(nc, in_maps)
        xs = [res[c]["x_out"] for c in range(8)]
        if gla:
            sg = [res[c]["gsg"] for c in range(8)]
            in_maps = []
            for c in range(8):
                b, hd = c // 4, c % 4
                src = [res[b * 4 + q] for q in range(4)]
                in_maps.append({
                    "qT": np.ascontiguousarray(np.concatenate([s["gq"][hd * 128:(hd + 1) * 128, :] for s in src], 1)),
                    "kT": np.ascontiguousarray(np.concatenate([s["gkT"][hd * 128:(hd + 1) * 128, :] for s in src], 1)),
                    "k": np.ascontiguousarray(np.concatenate([s["gk"][:, hd * 128:(hd + 1) * 128] for s in src], 0)),
                    "v": np.ascontiguousarray(np.concatenate([s["gv"][:, hd * 256:(hd + 1) * 256] for s in src], 0)),
                    "la": np.ascontiguousarray(np.concatenate([s["gla"][:, hd * 128:(hd + 1) * 128] for s in src], 0)),
                    "gcst": gcst,
                    "onorm": np.ascontiguousarray(f32(gla_o_norm[j]).reshape(2, 128).T),
                })
            ro = _run(_prog("gla", build_gla), in_maps)
            nrow = 256
        else:
            in_maps = []
            for c in range(8):
                b, g = c // 4, c % 4
                src = [res[b * 4 + q] for q in range(4)]
                in_maps.append({
                    "qT": np.ascontiguousarray(np.concatenate([s["oq"][g * 256:(g + 1) * 256, :] for s in src], 1)),
                    "kT": np.ascontiguousarray(np.concatenate([s["ok"][g * 256:(g + 1) * 256, :] for s in src], 1)),
                    "v": np.ascontiguousarray(np.concatenate([s["ov"][:, g * 256:(g + 1) * 256] for s in src], 0)),
                    "masks": masks, "tri": tri,
                })
            ro = _run(_prog("sb", build_sb), in_maps)
            nrow = 256
        mixer_out = []
        for c in range(8):
            b, q = c // 4, c % 4
            mixer_out.append(np.ascontiguousarray(np.concatenate(
                [ro[b * 4 + g]["oT"][:, q * TOK:(q + 1) * TOK] for g in range(4)], 0)))
    phases = ["out_sb", "ffn2", "final"]
    m = {"w_out": f32(sb_w_out[1]), "ffn2_g": f32(ffn2_w_gate[3]), "ffn2_u": f32(ffn2_w_up[3]), "ffn2_d": f32(ffn2_w_down[3])}
    gc = [_cols(ffn1_norm[3]), _cols(mix_norm[3]), _cols(ffn2_norm[3]), _cols(final_norm)]
    m["gcols_in"] = np.ascontiguousarray(np.concatenate(gc, 1))
    nc = _prog(tuple(phases), lambda: build_tok(phases))
    in_maps = []
    for c in range(8):
        mm = dict(m)
        mm["x_in"] = xs[c]
        mm["oin"] = mixer_out[c]
        in_maps.append(mm)
    res = _run(nc, in_maps)
    out = np.empty((2, SEQ, D), np.float32)
    for c in range(8):
        out[c // 4, (c % 4) * TOK:(c % 4 + 1) * TOK, :] = res[c]["y"].T
    return out
```
